# Optimizing a Trainium2 kernel written in Bass

```python
import jax, jax.numpy as jnp
from jax import lax
import numpy as np

D_MODEL = 1024
BATCH = 8
SEQ = 2048
DEPTH = 4

MIX_WIDTH = D_MODEL
POOL_WIDTH = MIX_WIDTH // 4
POOL_WINDOWS = (2, 4, 8, 16)
POOL_GROUPS = len(POOL_WINDOWS)
POOL_GROUP_DIM = POOL_WIDTH // POOL_GROUPS
HEAD_DIM = 64
ATTN_WIDTH = MIX_WIDTH - POOL_WIDTH
N_HEADS = ATTN_WIDTH // HEAD_DIM
DILATED_PATTERNS = ((128, 1), (512, 4), (2048, 16))
ROPE_THETA = 500000.0
ROPE_DIM = HEAD_DIM // 4
D_FF = 2816
IN_PROJ_WIDTH = POOL_WIDTH + 3 * ATTN_WIDTH
NORM_EPS = 1e-6
MASK_VALUE = -1e30

kernel_name = "hybrid_pool_dilated_attn_macaron_encoder"


def rmsnorm(x, g):
    xf = x.astype(jnp.float32)
    y = xf * lax.rsqrt(jnp.mean(xf * xf, axis=-1, keepdims=True) + NORM_EPS)
    return (y * g.astype(jnp.float32)).astype(x.dtype)


def swiglu(h, w_gate, w_up, w_down):
    return (jax.nn.silu(h @ w_gate) * (h @ w_up)) @ w_down


def rope_tables(positions):
    inv_freq = ROPE_THETA ** (-jnp.arange(0, ROPE_DIM, 2, dtype=jnp.float32) / ROPE_DIM)
    ang = positions.astype(jnp.float32)[..., None] * inv_freq
    return jnp.cos(ang)[:, :, None, :], jnp.sin(ang)[:, :, None, :]


def apply_partial_rope(t, cos, sin):
    tf = t.astype(jnp.float32)
    half = ROPE_DIM // 2
    t1, t2, rest = tf[..., :half], tf[..., half:ROPE_DIM], tf[..., ROPE_DIM:]
    rot = jnp.concatenate([t1 * cos - t2 * sin, t2 * cos + t1 * sin, rest], axis=-1)
    return rot.astype(t.dtype)


def multiscale_pool(v, pool_w, pool_scale):
    B, S, _ = v.shape
    vf = v.astype(jnp.float32).reshape(B, S, POOL_GROUPS, POOL_GROUP_DIM)
    cs = jnp.pad(lax.cumsum(vf, axis=1), ((0, 0), (1, 0), (0, 0), (0, 0)))
    pos = jnp.arange(S)
    means = []
    for g, w in enumerate(POOL_WINDOWS):
        lo = jnp.maximum(pos - w // 2, 0)
        hi = jnp.minimum(pos + w - 1 - w // 2, S - 1)
        cnt = (hi - lo + 1).astype(jnp.float32)
        means.append((cs[:, hi + 1, g] - cs[:, lo, g]) / cnt[None, :, None])
    pooled = jnp.stack(means, axis=2)
    diff = (pooled - vf).astype(v.dtype)
    y = jnp.einsum('bsgc,gcd->bsgd', diff, pool_w).reshape(B, S, POOL_WIDTH)
    return y * pool_scale


def dilated_branch(q, k, v, window, dilation):
    B, S, H, Dh = q.shape
    half = window // (2 * dilation)
    blk = half
    L = S // dilation
    nb = -(-L // blk)
    Lp = nb * blk

    def to_compressed(t):
        t = t.astype(jnp.float32).reshape(B, L, dilation, H, Dh)
        return jnp.pad(t, ((0, 0), (0, Lp - L), (0, 0), (0, 0), (0, 0)))

    def band(t):
        tp = jnp.pad(t, ((0, 0), (blk, blk), (0, 0), (0, 0), (0, 0)))
        tp = tp.reshape(B, nb + 2, blk, dilation, H, Dh)
        return jnp.concatenate([tp[:, :-2], tp[:, 1:-1], tp[:, 2:]], axis=2)

    qb = to_compressed(q).reshape(B, nb, blk, dilation, H, Dh)
    kb = band(to_compressed(k))
    vb = band(to_compressed(v))

    t_idx = jnp.arange(nb)[:, None] * blk + jnp.arange(blk)[None, :]
    j_idx = jnp.arange(nb)[:, None] * blk - blk + jnp.arange(3 * blk)[None, :]
    jj = j_idx[:, None, :]
    valid = (jnp.abs(jj - t_idx[:, :, None]) <= half) & (jj >= 0) & (jj < L)

    scale = 1.0 / np.sqrt(Dh)
    s = jnp.einsum('bnqrhd,bnkrhd->bnrhqk', qb, kb) * scale
    s = jnp.where(valid[None, :, None, None], s, MASK_VALUE)
    m = jnp.max(s, axis=-1, keepdims=True)
    p = jnp.exp(s - m)
    denom = jnp.sum(p, axis=-1)
    num = jnp.einsum('bnrhqk,bnkrhd->bnqrhd', p, vb)

    num = num.reshape(B, Lp, dilation, H, Dh)[:, :L].reshape(B, S, H, Dh)

    def stat_back(t):
        t = jnp.transpose(t, (0, 1, 4, 2, 3)).reshape(B, Lp, dilation, H)
        return t[:, :L].reshape(B, S, H)

    return num, stat_back(m[..., 0]), stat_back(denom)


def dilated_mixture_attention(q, k, v):
    branches = [dilated_branch(q, k, v, w, d) for (w, d) in DILATED_PATTERNS]
    m_all = jnp.stack([b[1] for b in branches], axis=0)
    wts = jnp.exp(m_all - jnp.max(m_all, axis=0, keepdims=True))
    num = sum(wts[i][..., None] * branches[i][0] for i in range(len(branches)))
    den = sum(wts[i] * branches[i][2] for i in range(len(branches)))
    return (num / den[..., None]).astype(q.dtype)


def setup_inputs(seed: int = 0) -> dict:
    key = jax.random.key(seed)
    ks = jax.random.split(key, 20)
    f32 = jnp.float32

    def normal(k, shape, fan_in):
        return jax.random.normal(k, shape, f32) * (fan_in ** -0.5)

    def gain(k, shape):
        return jnp.ones(shape, f32) + 0.02 * jax.random.normal(k, shape, f32)

    x = jax.random.normal(ks[0], (BATCH, SEQ, D_MODEL), f32)
    start = jax.random.randint(ks[1], (BATCH, 1), 0, 4096, dtype=jnp.int32)
    positions = start + jnp.arange(SEQ, dtype=jnp.int32)[None, :]
    return {
        "x": x,
        "positions": positions,
        "ffn1_norm": gain(ks[2], (DEPTH, D_MODEL)),
        "ffn1_w_gate": normal(ks[3], (DEPTH, D_MODEL, D_FF), D_MODEL),
        "ffn1_w_up": normal(ks[4], (DEPTH, D_MODEL, D_FF), D_MODEL),
        "ffn1_w_down": normal(ks[5], (DEPTH, D_FF, D_MODEL), D_FF),
        "mix_norm": gain(ks[6], (DEPTH, D_MODEL)),
        "w_in": normal(ks[7], (DEPTH, D_MODEL, IN_PROJ_WIDTH), D_MODEL),
        "pool_w": normal(ks[8], (DEPTH, POOL_GROUPS, POOL_GROUP_DIM, POOL_GROUP_DIM), POOL_GROUP_DIM),
        "pool_scale": gain(ks[9], (DEPTH, POOL_WIDTH)),
        "w_out": normal(ks[10], (DEPTH, MIX_WIDTH, D_MODEL), MIX_WIDTH),
        "ffn2_norm": gain(ks[11], (DEPTH, D_MODEL)),
        "ffn2_w_gate": normal(ks[12], (DEPTH, D_MODEL, D_FF), D_MODEL),
        "ffn2_w_up": normal(ks[13], (DEPTH, D_MODEL, D_FF), D_MODEL),
        "ffn2_w_down": normal(ks[14], (DEPTH, D_FF, D_MODEL), D_FF),
        "final_norm": gain(ks[15], (D_MODEL,)),
    }


def reference(x, positions, ffn1_norm, ffn1_w_gate, ffn1_w_up, ffn1_w_down, mix_norm, w_in,
              pool_w, pool_scale, w_out, ffn2_norm, ffn2_w_gate, ffn2_w_up, ffn2_w_down, final_norm):
    B, S, _ = x.shape
    cos, sin = rope_tables(positions)
    for l in range(DEPTH):
        x = x + 0.5 * swiglu(rmsnorm(x, ffn1_norm[l]), ffn1_w_gate[l], ffn1_w_up[l], ffn1_w_down[l])

        h = rmsnorm(x, mix_norm[l])
        proj = h @ w_in[l]
        v_pool = proj[..., :POOL_WIDTH]
        q = proj[..., POOL_WIDTH:POOL_WIDTH + ATTN_WIDTH].reshape(B, S, N_HEADS, HEAD_DIM)
        k = proj[..., POOL_WIDTH + ATTN_WIDTH:POOL_WIDTH + 2 * ATTN_WIDTH].reshape(B, S, N_HEADS, HEAD_DIM)
        v = proj[..., POOL_WIDTH + 2 * ATTN_WIDTH:].reshape(B, S, N_HEADS, HEAD_DIM)

        y_pool = multiscale_pool(v_pool, pool_w[l], pool_scale[l])
        q = apply_partial_rope(q, cos, sin)
        k = apply_partial_rope(k, cos, sin)
        y_attn = dilated_mixture_attention(q, k, v).reshape(B, S, ATTN_WIDTH)

        mixed = jnp.concatenate([y_pool.astype(x.dtype), y_attn.astype(x.dtype)], axis=-1)
        x = x + mixed @ w_out[l]

        x = x + 0.5 * swiglu(rmsnorm(x, ffn2_norm[l]), ffn2_w_gate[l], ffn2_w_up[l], ffn2_w_down[l])
    return rmsnorm(x, final_norm)
```

```python
import numpy as np
import concourse.bass as bass
import concourse.mybir as mybir
from concourse.bass_utils import run_bass_kernel_spmd

F32 = mybir.dt.float32
BF16 = mybir.dt.bfloat16
I32 = mybir.dt.int32
ALU = mybir.AluOpType
AF = mybir.ActivationFunctionType

D = 1024
S = 2048
KC = 8
DFF = 2816
FC = 22
NT = 4
DEPTH = 4
F_GROUPS = [8, 7, 7]
NSLOT = 4
EPS = 1e-6
ROPE_THETA = 500000.0
POOL_WINDOWS = (2, 4, 8, 16)

C_INVF, C_SGN, C_HALFPI, C_INVW, C_INVCNT, C_PSCALE, C_GAIN = 0, 1, 2, 3, 5, 37, 45
NCONST = C_GAIN + 13 * 8

SCRN = 20480
Q2_OFF, K2_OFF = 16384, 18432
NTMPF = 3
Q_OFF, K_OFF, V_OFF, P_OFF, MIX_OFF = 0, 2048, 4096, 12288, 14336
VP_OFF, TA_OFF, TB_OFF = 0, 4128, 8256
PADW = 2064


class _Op:
    __slots__ = ("eng", "fn", "deps", "is_dma", "dsem", "dcount", "signal", "sidx", "is_mm")

    def __init__(self, eng, fn, is_mm):
        self.eng = eng
        self.fn = fn
        self.deps = []
        self.is_dma = False
        self.dsem = None
        self.dcount = 0
        self.signal = False
        self.sidx = 0
        self.is_mm = is_mm


class Sched:
    ENGS = ("pe", "act", "dve", "pool", "sp")

    def __init__(self):
        self.ops = {e: [] for e in self.ENGS}
        self.res = {}
        self.dma_counts = {}

    def _dep(self, op, b):
        if b is op or b is None:
            return
        if b.is_dma:
            op.deps.append(b)
            return
        if b.eng == op.eng and op.eng == "pe" and op.is_mm and b.is_mm:
            return
        b.signal = True
        op.deps.append(b)

    def add(self, eng, fn, reads=(), writes=(), dma=None, is_mm=False):
        op = _Op(eng, fn, is_mm)
        for k in reads:
            r = self.res.get(k)
            if r is not None:
                self._dep(op, r[0])
        for k in writes:
            r = self.res.get(k)
            if r is not None:
                self._dep(op, r[0])
                for rd in r[1].values():
                    self._dep(op, rd)
        if dma is not None:
            cnt = self.dma_counts.get(dma, 0) + 1
            self.dma_counts[dma] = cnt
            op.is_dma = True
            op.dsem = dma
            op.dcount = 16 * cnt
        rkey = ("d", dma) if dma is not None else ("e", eng)
        for k in reads:
            r = self.res.get(k)
            if r is None:
                r = [None, {}]
                self.res[k] = r
            r[1][rkey] = op
        for k in writes:
            self.res[k] = [op, {}]
        self.ops[eng].append(op)
        return op

    def emit(self, nc, block, esems, dsems):
        for e in self.ENGS:
            n = 0
            for op in self.ops[e]:
                if op.signal and not op.is_dma:
                    n += 1
                    op.sidx = n
        decos = {"pe": block.tensor, "act": block.scalar, "dve": block.vector,
                 "pool": block.gpsimd, "sp": block.sync}
        for e in self.ENGS:
            ops = self.ops[e]

            def body(eng, e=e, ops=ops):
                known = {}
                for op in ops:
                    need = {}
                    for b in op.deps:
                        if b.is_dma:
                            key, val = ("d", b.dsem), b.dcount
                        else:
                            key, val = ("e", b.eng), b.sidx
                        if val > need.get(key, 0):
                            need[key] = val
                    for key, val in need.items():
                        if known.get(key, 0) >= val:
                            continue
                        sem = dsems[key[1]] if key[0] == "d" else esems[key[1]]
                        eng.wait_ge(sem, val)
                        known[key] = val
                    ins = op.fn(eng)
                    if ins is None:
                        continue
                    if op.is_dma:
                        ins.then_inc(dsems[op.dsem], 16)
                    elif op.signal:
                        ins.then_inc(esems[e], 1)

            decos[e](body)


def build_program(NL=DEPTH, stop_after=None, apply_final=True):
    nc = bass.Bass("TRN2", target_bir_lowering=False)
    dt_in = lambda name, shape, dt=F32: nc.dram_tensor(name, shape, dt, kind="ExternalInput").ap()
    xT_d = dt_in("xT", [128, KC * S])
    pos_d = dt_in("pos", [1, S], I32)
    const_d = dt_in("consts", [128, NCONST])
    mask_d = dt_in("masks", [128, 1152])
    poolbd_d = dt_in("poolbd", [128, DEPTH * 2 * 128])
    wgu_d = dt_in("wgu", [DEPTH * 2 * FC * 128, 2048])
    wd_d = dt_in("wd", [DEPTH * 2 * 3 * 8 * 128, 1024])
    wqk_d = dt_in("wqk", [DEPTH * 6 * 2 * 128, 1024])
    wv_d = dt_in("wv", [DEPTH * 6 * 128, 1024])
    wpool_d = dt_in("wpool", [DEPTH * 2 * 128, 1024])
    wout_d = dt_in("wout", [DEPTH * 8 * 128, 1024])
    out_d = nc.dram_tensor("outT", [128, KC * S], F32, kind="ExternalOutput").ap()

    sch = Sched()
    from contextlib import ExitStack
    with ExitStack() as es:
        sb = lambda name, shape, dt: es.enter_context(nc.sbuf_tensor(name, shape, dt))
        X = sb("X", [128, KC, S], F32)
        H = sb("H", [128, KC, S], BF16)
        SCR = sb("SCR", [128, SCRN], BF16)
        ACC = sb("ACC", [128, 2, S], F32)
        ROPC = sb("ROPC", [128, S], F32)
        ROPS = sb("ROPS", [128, S], F32)
        WR = sb("WR", [128, NSLOT, 2048], BF16)
        SQ = sb("SQ", [128, 2, 512], BF16)
        TMPF = sb("TMPF", [128, NTMPF, 512], F32)
        RSTD = sb("RSTD", [128, 512], F32)
        CONST = sb("CONST", [128, NCONST], F32)
        MASKS = sb("MASKS", [128, 1152], BF16)
        ONES = sb("ONES", [128, 128], BF16)
        POOLBD = sb("POOLBD", [128, DEPTH * 2 * 128], BF16)
        VT0 = sb("VT0", [128, S], BF16)
        VT1 = sb("VT1", [128, S], BF16)
        POSI = SQ[:, :, :].rearrange("p a b -> p (a b)").bitcast(I32)
        PS = [es.enter_context(nc.psum_tensor(f"ps{i}", [128, 512], F32)) for i in range(8)]

        dsem_names = [f"w{i}" for i in range(NSLOT)] + ["xin", "cst0", "cst1", "cst2", "posi", "out"]
        esems = {e: es.enter_context(nc.semaphore(f"sem_{e}")) for e in Sched.ENGS}
        dsems = {n: es.enter_context(nc.semaphore(f"dsem_{n}")) for n in dsem_names}

        def cc(col, n=1):
            return CONST[:, col:col + n]

        jobs = []

        def rows(d, idx):
            return d[idx * 128:(idx + 1) * 128, :]

        def ffn_jobs(l, f):
            j0 = 0
            for g, ng in enumerate(F_GROUPS):
                for j in range(j0, j0 + ng):
                    jobs.append((rows(wgu_d, (l * 2 + f) * FC + j), 2048, "gu"))
                for dc in range(8):
                    jobs.append((rows(wd_d, ((l * 2 + f) * 3 + g) * 8 + dc), ng * 128, "wd"))
                j0 += ng

        def mix_jobs(l):
            for c2 in range(2):
                jobs.append((rows(wpool_d, l * 2 + c2), 1024, "pool"))
            def qkv(p):
                jobs.append((rows(wqk_d, (l * 6 + p) * 2 + 0), 1024, "qk"))
                jobs.append((rows(wqk_d, (l * 6 + p) * 2 + 1), 1024, "qk"))
                jobs.append((rows(wv_d, l * 6 + p), 1024, "v"))
            qkv(0)
            for p in range(6):
                if p == 0:
                    jobs.append((rows(wout_d, l * 8 + 0), 1024, "wout"))
                    jobs.append((rows(wout_d, l * 8 + 1), 1024, "wout"))
                if p > 0:
                    jobs.append((rows(wout_d, l * 8 + 2 + p - 1), 1024, "wout"))
                if p < 5:
                    qkv(p + 1)
            jobs.append((rows(wout_d, l * 8 + 2 + 5), 1024, "wout"))

        for l in range(NL):
            ffn_jobs(l, 0)
            if stop_after == ("ffn1", l):
                break
            mix_jobs(l)
            if stop_after == ("mix", l):
                break
            ffn_jobs(l, 1)
        st = {"issued": 0, "next": 0}
        PF = NSLOT - 1

        def issue_job(ji):
            dram, ncols, kind = jobs[ji]
            slot = ji % NSLOT
            sch.add("pool",
                    lambda eng, slot=slot, dram=dram, ncols=ncols:
                    eng.dma_start(out=WR[:, slot, 0:ncols], in_=dram[:, 0:ncols]),
                    writes=[("w", slot), ("ws", slot)], dma=f"w{slot}")
            if kind == "qk":
                sch.add("pool", lambda eng, slot=slot: eng.memset(WR[:, slot, 1024:2048], 0.0),
                        writes=[("ws", slot)])
                main = WR[:, slot, 0:1024].rearrange("p (k c) -> p k c", c=128)
                swp = WR[:, slot, 1024:2048].rearrange("p (k c) -> p k c", c=128)
                for hh in range(2):
                    b = hh * 64
                    sch.add("pool", lambda eng, o=swp[:, :, b:b + 8], i=main[:, :, b + 8:b + 16]:
                            eng.tensor_copy(out=o, in_=i), reads=[("w", slot)], writes=[("ws", slot)])
                    sch.add("pool", lambda eng, o=swp[:, :, b + 8:b + 16], i=main[:, :, b:b + 8]:
                            eng.tensor_copy(out=o, in_=i), reads=[("w", slot)], writes=[("ws", slot)])

        def next_job(kind, hold=0):
            ji = st["next"]
            assert jobs[ji][2] == kind, (jobs[ji][2], kind, ji)
            while st["issued"] < min(len(jobs), ji + PF + 1 - hold):
                issue_job(st["issued"])
                st["issued"] += 1
            st["next"] += 1
            return ji % NSLOT

        sch.add("sp", lambda eng: eng.dma_start(out=CONST[:, :], in_=const_d), writes=[("const",)], dma="cst0")
        sch.add("sp", lambda eng: eng.dma_start(out=X[:, :, :].rearrange("p k t -> p (k t)"), in_=xT_d),
                writes=[("x", kc, tt) for kc in range(KC) for tt in range(NT)], dma="xin")
        sch.add("pool", lambda eng: eng.dma_start(out=MASKS[:, :], in_=mask_d), writes=[("masks",)], dma="cst1")
        sch.add("pool", lambda eng: eng.dma_start(out=POOLBD[:, :], in_=poolbd_d), writes=[("poolbd",)], dma="cst2")
        sch.add("pool", lambda eng: eng.memset(ONES[:, :], 1.0), writes=[("ones",)])

        TWO_PI = float(2.0 * np.pi)
        C1 = 6.28125
        C2 = float(2.0 * np.pi - 6.28125)
        for tt in range(NT):
            ts = slice(tt * 512, (tt + 1) * 512)
            sch.add("sp", lambda eng, ts=ts: eng.dma_start(out=POSI[:, :], in_=pos_d[:, ts].broadcast_to([128, 512])),
                    writes=[("posi",), ("sq", 0), ("sq", 1)], dma="posi")
            ang = ROPC[:, ts]
            kf = ROPS[:, ts]
            sch.add("dve", lambda eng, ang=ang: eng.tensor_copy(out=ang, in_=POSI[:, :]),
                    reads=[("posi",)], writes=[("ropc", tt)])
            sch.add("dve", lambda eng, ang=ang: eng.tensor_scalar(out=ang, in0=ang, scalar1=cc(C_INVF), scalar2=None, op0=ALU.mult),
                    reads=[("const",), ("ropc", tt)], writes=[("ropc", tt)])
            t1 = TMPF[:, 0, :]
            sch.add("dve", lambda eng, ang=ang, t1=t1: eng.tensor_scalar(out=t1, in0=ang, scalar1=float(1.0 / TWO_PI), scalar2=None, op0=ALU.mult),
                    reads=[("ropc", tt)], writes=[("tmpf", 0)])
            ki = POSI[:, :]
            sch.add("dve", lambda eng, t1=t1, ki=ki: eng.tensor_copy(out=ki, in_=t1),
                    reads=[("tmpf", 0)], writes=[("posi",), ("sq", 0), ("sq", 1)])
            sch.add("dve", lambda eng, kf=kf, ki=ki: eng.tensor_copy(out=kf, in_=ki),
                    reads=[("posi",)], writes=[("rops", tt)])
            sch.add("dve", lambda eng, kf=kf, ang=ang: eng.scalar_tensor_tensor(out=ang, in0=kf, scalar=-C1, in1=ang, op0=ALU.mult, op1=ALU.add),
                    reads=[("rops", tt), ("ropc", tt)], writes=[("ropc", tt)])
            sch.add("dve", lambda eng, kf=kf, ang=ang: eng.scalar_tensor_tensor(out=ang, in0=kf, scalar=-C2, in1=ang, op0=ALU.mult, op1=ALU.add),
                    reads=[("rops", tt), ("ropc", tt)], writes=[("ropc", tt)])
            sch.add("dve", lambda eng, ang=ang: eng.tensor_scalar(out=ang, in0=ang, scalar1=float(np.pi), scalar2=float(-np.pi), op0=ALU.min, op1=ALU.max),
                    reads=[("ropc", tt)], writes=[("ropc", tt)])
            sch.add("act", lambda eng, kf=kf, ang=ang: eng.activation(out=kf, in_=ang, func=AF.Sin),
                    reads=[("ropc", tt)], writes=[("rops", tt)])
            sch.add("dve", lambda eng, kf=kf: eng.tensor_scalar(out=kf, in0=kf, scalar1=cc(C_SGN), scalar2=None, op0=ALU.mult),
                    reads=[("rops", tt), ("const",)], writes=[("rops", tt)])
            t2 = TMPF[:, 1, :]
            sch.add("act", lambda eng, ang=ang, t2=t2: eng.activation(out=t2, in_=ang, func=AF.Sin, scale=0.5),
                    reads=[("ropc", tt)], writes=[("tmpf", 1)])
            sch.add("dve", lambda eng, t2=t2: eng.tensor_tensor(out=t2, in0=t2, in1=t2, op=ALU.mult),
                    reads=[("tmpf", 1)], writes=[("tmpf", 1)])
            sch.add("dve", lambda eng, ang=ang, t2=t2: eng.tensor_scalar(out=ang, in0=t2, scalar1=-2.0, scalar2=1.0, op0=ALU.mult, op1=ALU.add),
                    reads=[("tmpf", 1), ("rops", tt)], writes=[("ropc", tt)])

        cnt = {"sq": 0, "tmpf": 0, "gate": 0, "down": 0}
        BARR = sb("BARR", [128, 8], F32)
        SCR_KEYS = ([("act", jj, tt) for jj in range(max(F_GROUPS)) for tt in range(NT)]
                    + [("vp",), ("ta",), ("tb",), ("diff",)]
                    + [(("q", qb), t) for qb in range(2) for t in range(NT)] + [(("k", qb), t) for qb in range(2) for t in range(NT)]
                    + [("V", vb, t) for vb in range(2) for t in range(16)]
                    + [("P", i) for i in range(4)] + [("mixed", t) for t in range(NT)])

        def scr_barrier():
            sch.add("pool", lambda eng: eng.memset(BARR[:, :], 0.0), writes=SCR_KEYS)

        def emit_norm(gidx):
            for tt in range(NT):
                ts = slice(tt * 512, (tt + 1) * 512)
                for kc in range(KC):
                    i = cnt["sq"] % 2
                    cnt["sq"] += 1
                    sch.add("act", lambda eng, i=i, kc=kc, ts=ts: eng.activation(out=SQ[:, i, :], in_=X[:, kc, ts], func=AF.Square),
                            reads=[("x", kc, tt)], writes=[("sq", i)])
                    sch.add("pe", lambda eng, i=i, kc=kc: eng.matmul(PS[6][:, :], ONES[:, :], SQ[:, i, :], start=(kc == 0), stop=(kc == KC - 1)),
                            reads=[("sq", i), ("ones",)], writes=[("ps", 6)], is_mm=True)
                sch.add("dve", lambda eng: eng.tensor_scalar(out=RSTD[:, :], in0=PS[6][:, :], scalar1=float(1.0 / D), scalar2=float(EPS), op0=ALU.mult, op1=ALU.add),
                        reads=[("ps", 6)], writes=[("rstd",)])
                sch.add("act", lambda eng: eng.activation(out=RSTD[:, :], in_=RSTD[:, :], func=AF.Sqrt),
                        reads=[("rstd",)], writes=[("rstd",)])
                sch.add("dve", lambda eng: eng.reciprocal(out=RSTD[:, :], in_=RSTD[:, :]),
                        reads=[("rstd",)], writes=[("rstd",)])
                for kc in range(KC):
                    sch.add("dve", lambda eng, kc=kc, ts=ts: eng.scalar_tensor_tensor(
                        out=H[:, kc, ts], in0=X[:, kc, ts], scalar=cc(C_GAIN + gidx * 8 + kc), in1=RSTD[:, :], op0=ALU.mult, op1=ALU.mult),
                        reads=[("x", kc, tt), ("rstd",), ("const",)], writes=[("h", kc, tt)])

        def emit_ffn(l, f):
            gidx = l * 3 + (0 if f == 0 else 2)
            scr_barrier()
            emit_norm(gidx)
            j0 = 0
            for g, ng in enumerate(F_GROUPS):
                for jj in range(ng):
                    slot = next_job("gu")
                    for tt in range(NT):
                        ts = slice(tt * 512, (tt + 1) * 512)
                        gi = cnt["gate"] % 2
                        cnt["gate"] += 1
                        pg, pu = PS[gi], PS[2 + gi]
                        for kc in range(KC):
                            sch.add("pe", lambda eng, pg=pg, slot=slot, kc=kc, ts=ts: eng.matmul(
                                pg[:, :], WR[:, slot, kc * 128:(kc + 1) * 128], H[:, kc, ts], start=(kc == 0), stop=(kc == KC - 1)),
                                reads=[("w", slot), ("h", kc, tt)], writes=[("ps", gi)], is_mm=True)
                        for kc in range(KC):
                            sch.add("pe", lambda eng, pu=pu, slot=slot, kc=kc, ts=ts: eng.matmul(
                                pu[:, :], WR[:, slot, 1024 + kc * 128:1024 + (kc + 1) * 128], H[:, kc, ts], start=(kc == 0), stop=(kc == KC - 1)),
                                reads=[("w", slot), ("h", kc, tt)], writes=[("ps", 2 + gi)], is_mm=True)
                        ti = cnt["tmpf"] % NTMPF
                        cnt["tmpf"] += 1
                        sch.add("act", lambda eng, ti=ti, pg=pg: eng.activation(out=TMPF[:, ti, :], in_=pg[:, :], func=AF.Silu),
                                reads=[("ps", gi)], writes=[("tmpf", ti)])
                        sch.add("dve", lambda eng, ti=ti, pu=pu, jj=jj, ts=ts: eng.tensor_tensor(
                            out=SCR[:, jj * S + ts.start: jj * S + ts.stop], in0=pu[:, :], in1=TMPF[:, ti, :], op=ALU.mult),
                            reads=[("ps", 2 + gi), ("tmpf", ti)], writes=[("act", jj, tt)])
                for dc in range(8):
                    slot = next_job("wd")
                    for tt in range(NT):
                        ts = slice(tt * 512, (tt + 1) * 512)
                        di = cnt["down"] % 2
                        cnt["down"] += 1
                        pd = PS[4 + di]
                        for jj in range(ng):
                            sch.add("pe", lambda eng, pd=pd, slot=slot, jj=jj, ts=ts, last=(jj == ng - 1): eng.matmul(
                                pd[:, :], WR[:, slot, jj * 128:(jj + 1) * 128], SCR[:, jj * S + ts.start: jj * S + ts.stop],
                                start=(jj == 0), stop=last),
                                reads=[("w", slot), ("act", jj, tt)], writes=[("ps", 4 + di)], is_mm=True)
                        sch.add("dve", lambda eng, pd=pd, dc=dc, ts=ts: eng.scalar_tensor_tensor(
                            out=X[:, dc, ts], in0=pd[:, :], scalar=0.5, in1=X[:, dc, ts], op0=ALU.mult, op1=ALU.add),
                            reads=[("ps", 4 + di), ("x", dc, tt)], writes=[("x", dc, tt)])
                j0 += ng

        ocnt = {"o": 0}

        def gen_outproj(slot, banks=(4, 5)):
            for dc in range(8):
                for tt in range(NT):
                    ts = slice(tt * 512, (tt + 1) * 512)
                    oi = banks[ocnt["o"] % len(banks)]
                    ocnt["o"] += 1
                    po = PS[oi]
                    sch.add("pe", lambda eng, po=po, slot=slot, dc=dc, ts=ts: eng.matmul(
                        po[:, :], WR[:, slot, dc * 128:(dc + 1) * 128], SCR[:, MIX_OFF + ts.start: MIX_OFF + ts.stop], start=True, stop=True),
                        reads=[("w", slot), ("mixed", tt)], writes=[("ps", oi)], is_mm=True)
                    ti = cnt["tmpf"] % NTMPF
                    cnt["tmpf"] += 1
                    sch.add("act", lambda eng, po=po, ti=ti: eng.activation(out=TMPF[:, ti, :], in_=po[:, :], func=AF.Copy),
                            reads=[("ps", oi)], writes=[("tmpf", ti)])
                    sch.add("pool", lambda eng, ti=ti, dc=dc, ts=ts: eng.tensor_tensor(
                        out=X[:, dc, ts], in0=TMPF[:, ti, :], in1=X[:, dc, ts], op=ALU.add),
                        reads=[("tmpf", ti), ("x", dc, tt)], writes=[("x", dc, tt)])
                    yield

        def emit_outproj(slot):
            for _ in gen_outproj(slot):
                pass

        def emit_pool(l):
            VP = SCR[:, VP_OFF:VP_OFF + 2 * PADW].bitcast(F32)
            TA = SCR[:, TA_OFF:TA_OFF + 2 * PADW].bitcast(F32)
            TB = SCR[:, TB_OFF:TB_OFF + 2 * PADW].bitcast(F32)
            DIFF = SCR[:, TB_OFF:TB_OFF + S]
            for c2 in range(2):
                scr_barrier()
                slot = next_job("pool")
                sch.add("pool", lambda eng: eng.memset(VP[:, 0:8], 0.0), writes=[("vp",)])
                sch.add("pool", lambda eng: eng.memset(VP[:, 8 + S:PADW], 0.0), writes=[("vp",)])
                for tt in range(NT):
                    ts = slice(tt * 512, (tt + 1) * 512)
                    bi = tt % 2
                    for kc in range(KC):
                        sch.add("pe", lambda eng, bi=bi, slot=slot, kc=kc, ts=ts: eng.matmul(
                            PS[bi][:, :], WR[:, slot, kc * 128:(kc + 1) * 128], H[:, kc, ts], start=(kc == 0), stop=(kc == KC - 1)),
                            reads=[("w", slot), ("h", kc, tt)], writes=[("ps", bi)], is_mm=True)
                    sch.add("act", lambda eng, bi=bi, ts=ts: eng.activation(out=VP[:, 8 + ts.start:8 + ts.stop], in_=PS[bi][:, :], func=AF.Copy),
                            reads=[("ps", bi)], writes=[("vp",)])
                for half in range(2):
                    w = POOL_WINDOWS[c2 * 2 + half]
                    pr = slice(half * 64, half * 64 + 64)

                    def addop(dst, src, shift, n, pr=pr):
                        sch.add("dve", lambda eng, dst=dst, src=src, shift=shift, n=n, pr=pr: eng.tensor_tensor(
                            out=dst[pr, 0:n], in0=src[pr, 0:n], in1=src[pr, shift:shift + n], op=ALU.add),
                            reads=[("vp",), ("ta",), ("tb",)], writes=[("ta",), ("tb",)])

                    def finop(src, hw, pr=pr):
                        sch.add("dve", lambda eng, src=src, hw=hw, pr=pr: eng.tensor_tensor(
                            out=TA[pr, 0:S], in0=src[pr, 8 - hw:8 - hw + S], in1=src[pr, 8:8 + S], op=ALU.add),
                            reads=[("vp",), ("ta",), ("tb",)], writes=[("ta",), ("tb",)])

                    if w == 2:
                        finop(VP, 1)
                    elif w == 4:
                        addop(TB, VP, 1, PADW - 1)
                        finop(TB, 2)
                    elif w == 8:
                        addop(TA, VP, 1, PADW - 1)
                        addop(TB, TA, 2, PADW - 3)
                        finop(TB, 4)
                    else:
                        addop(TB, VP, 1, PADW - 1)
                        addop(TA, TB, 2, PADW - 3)
                        addop(TB, TA, 4, PADW - 7)
                        finop(TB, 8)
                sch.add("dve", lambda eng, c2=c2: eng.tensor_scalar(out=TA[:, 8:S - 8], in0=TA[:, 8:S - 8], scalar1=cc(C_INVW + c2), scalar2=None, op0=ALU.mult),
                        reads=[("ta",), ("const",)], writes=[("ta",)])
                sch.add("dve", lambda eng, c2=c2: eng.tensor_tensor(out=TA[:, 0:8], in0=TA[:, 0:8], in1=cc(C_INVCNT + c2 * 16, 8), op=ALU.mult),
                        reads=[("ta",), ("const",)], writes=[("ta",)])
                sch.add("dve", lambda eng, c2=c2: eng.tensor_tensor(out=TA[:, S - 8:S], in0=TA[:, S - 8:S], in1=cc(C_INVCNT + c2 * 16 + 8, 8), op=ALU.mult),
                        reads=[("ta",), ("const",)], writes=[("ta",)])
                sch.add("dve", lambda eng: eng.tensor_tensor(out=DIFF, in0=TA[:, 0:S], in1=VP[:, 8:8 + S], op=ALU.subtract),
                        reads=[("ta",), ("vp",), ("tb",)], writes=[("tb",), ("diff",)])
                for tt in range(NT):
                    ts = slice(tt * 512, (tt + 1) * 512)
                    bi = 2 + tt % 2
                    bd = POOLBD[:, (l * 2 + c2) * 128:(l * 2 + c2 + 1) * 128]
                    sch.add("pe", lambda eng, bi=bi, bd=bd, ts=ts: eng.matmul(PS[bi][:, :], bd, DIFF[:, ts], start=True, stop=True),
                            reads=[("poolbd",), ("diff",)], writes=[("ps", bi)], is_mm=True)
                    moff = MIX_OFF if c2 == 0 else Q2_OFF
                    mkey = ("mixed", tt) if c2 == 0 else (("q", 1), tt)
                    sch.add("dve", lambda eng, bi=bi, ts=ts, c2=c2, moff=moff: eng.tensor_scalar(
                        out=SCR[:, moff + ts.start:moff + ts.stop], in0=PS[bi][:, :], scalar1=cc(C_PSCALE + l * 2 + c2), scalar2=None, op0=ALU.mult),
                        reads=[("ps", bi), ("const",)], writes=[mkey])

        pcnt = {"p": 0, "s": 0, "o": 0, "v": 0, "t": 0}

        def tt_keys(name, lo, hi):
            return [(name, t) for t in range(lo // 512, (hi - 1) // 512 + 1)]

        def qk_off(qb):
            return (Q_OFF, K_OFF) if qb == 0 else (Q2_OFF, K2_OFF)

        IDENT = MASKS[:, 1024:1152]
        VTS = [VT0[:, :], VT1[:, :]]

        class Unit:
            __slots__ = ("qk", "sm", "pv")

            def __init__(self, qk, sm, pv):
                self.qk, self.sm, self.pv = qk, sm, pv

        def branch_units(hh, L, dil, vb, first, qb):
            out = []
            base = hh * 64
            qo, ko = qk_off(qb)
            QT = SCR[base:base + 64, qo:qo + S]
            KT = SCR[base:base + 64, ko:ko + S]
            qn, kn = ("q", qb), ("k", qb)
            allq = [(qn, t) for t in range(NT)]
            allk = [(kn, t) for t in range(NT)]
            ntile = L // 128
            for r in range(dil):
                def tok(tc0, n, r=r):
                    return slice(r + dil * tc0, r + dil * (tc0 + n - 1) + 1, dil)

                units = [(a, min(a + 2, ntile)) for a in range(0, ntile, 2)]
                ngroups = (ntile + 1 + 3) // 4
                state = {"evd": set()}

                def qk(u, tok=tok, state=state, units=units):
                    if "obase" not in state:
                        state["obase"] = pcnt["o"]
                        pcnt["o"] += ngroups
                    a, b = units[u]
                    sbank = pcnt["s"] % 2
                    pcnt["s"] += 1
                    lo_col, hi_col = None, None
                    for kt in range(a, b):
                        c0 = (kt - a) * 256
                        qlo = max(0, 128 * kt - 64)
                        qhi = min(L, 128 * kt + 192)
                        col_lo = c0 + (qlo - (128 * kt - 64))
                        col_hi = col_lo + (qhi - qlo)
                        if lo_col is None:
                            lo_col = col_lo
                        hi_col = col_hi
                        kr = allk if dil > 1 else tt_keys(kn, 128 * kt, 128 * kt + 128)
                        qr = allq if dil > 1 else tt_keys(qn, qlo, qhi)
                        kap = KT[:, tok(128 * kt, 128)]
                        qap = QT[:, tok(qlo, qhi - qlo)]
                        sch.add("pe", lambda eng, sbank=sbank, col_lo=col_lo, col_hi=col_hi, kap=kap, qap=qap: eng.matmul(
                            PS[sbank][:, col_lo:col_hi], kap, qap, start=True, stop=True),
                            reads=kr + qr, writes=[("ps", sbank)], is_mm=True)
                    state[u] = (sbank, lo_col, hi_col)

                def sm(u, state=state):
                    sbank, lo_col, hi_col = state[u]
                    pi = pcnt["p"] % 4
                    pcnt["p"] += 1
                    state[("p", u)] = pi
                    Pt = SCR[:, P_OFF + pi * 512:P_OFF + (pi + 1) * 512]
                    sch.add("act", lambda eng, Pt=Pt, sbank=sbank, lo_col=lo_col, hi_col=hi_col: eng.activation(
                        out=Pt[:, lo_col:hi_col], in_=PS[sbank][:, lo_col:hi_col], func=AF.Exp, scale=0.125),
                        reads=[("ps", sbank)], writes=[("P", pi)])
                    sch.add("dve", lambda eng, Pt=Pt, lo_col=lo_col, hi_col=hi_col: eng.tensor_tensor(
                        out=Pt[:, lo_col:hi_col], in0=Pt[:, lo_col:hi_col], in1=MASKS[:, lo_col:hi_col], op=ALU.mult),
                        reads=[("P", pi), ("masks",)], writes=[("P", pi)])

                def evac(g, tok=tok, state=state):
                    if g in state["evd"]:
                        return
                    state["evd"].add(g)
                    obank = 2 + (state["obase"] + g) % 2
                    tlo = max(0, 512 * g - 64)
                    thi = min(L, 512 * g + 448)
                    oc_lo = tlo - (512 * g - 64)
                    n = thi - tlo
                    dst = ACC[:, hh, tok(tlo, n)]
                    src = PS[obank][:, oc_lo:oc_lo + n]
                    if first:
                        sch.add("act", lambda eng, dst=dst, src=src: eng.activation(out=dst, in_=src, func=AF.Copy),
                                reads=[("ps", obank)], writes=[("acc", hh)])
                    else:
                        sch.add("dve", lambda eng, dst=dst, src=src: eng.tensor_tensor(out=dst, in0=src, in1=dst, op=ALU.add),
                                reads=[("ps", obank), ("acc", hh)], writes=[("acc", hh)])

                def pv(u, r=r, state=state, evac=evac, units=units):
                    a, b = units[u]
                    pi = state[("p", u)]
                    Pt = SCR[:, P_OFF + pi * 512:P_OFF + (pi + 1) * 512]
                    for kt in range(a, b):
                        c0 = (kt - a) * 256
                        vt = (r * ntile + kt) if dil > 1 else kt
                        Vt = SCR[:, V_OFF + vb * 4096 + (vt * 2 + hh) * 128: V_OFF + vb * 4096 + (vt * 2 + hh + 1) * 128]
                        for side in range(2):
                            m = kt + side
                            tlo = max(0, 128 * m - 64)
                            thi = min(L, 128 * m + 64)
                            if thi <= tlo:
                                continue
                            pc_lo = c0 + side * 128 + (tlo - (128 * m - 64))
                            pc_hi = pc_lo + (thi - tlo)
                            g = m // 4
                            obank = 2 + (state["obase"] + g) % 2
                            oc_lo = (m % 4) * 128 + (tlo - (128 * m - 64))
                            oc_hi = oc_lo + (thi - tlo)
                            is_first = (side == 1) or (m == 0)
                            is_last = (side == 0) or (m == ntile)
                            sch.add("pe", lambda eng, obank=obank, oc_lo=oc_lo, oc_hi=oc_hi, Vt=Vt, Pt=Pt, pc_lo=pc_lo, pc_hi=pc_hi,
                                    is_first=is_first, is_last=is_last: eng.matmul(
                                PS[obank][:, oc_lo:oc_hi], Vt, Pt[:, pc_lo:pc_hi], start=is_first, stop=is_last, skip_group_check=True),
                                reads=[("P", pi), ("V", vb, vt)], writes=[("ps", obank)], is_mm=True)
                            if side == 0 and m % 4 == 3:
                                evac(g)
                        if kt == ntile - 1:
                            evac(ntile // 4)

                for u in range(len(units)):
                    out.append(Unit(lambda u=u, qk=qk: qk(u), lambda u=u, sm=sm: sm(u), lambda u=u, pv=pv: pv(u)))
            return out

        def branch3_units(hh, vb, qb):
            out = []
            base = hh * 64
            qo, ko = qk_off(qb)
            QT = SCR[base:base + 64, qo:qo + S]
            KT = SCR[base:base + 64, ko:ko + S]
            allq = [(("q", qb), t) for t in range(NT)]
            allk = [(("k", qb), t) for t in range(NT)]
            for g4 in range(4):
                st3 = {}

                def qk(g4=g4, st3=st3):
                    sbank = pcnt["s"] % 2
                    pcnt["s"] += 1
                    st3["s"] = sbank
                    for a in range(4):
                        r = g4 * 4 + a
                        sch.add("pe", lambda eng, sbank=sbank, a=a, kap=KT[:, r:S:16], qap=QT[:, r:S:16]: eng.matmul(
                            PS[sbank][:, a * 128:(a + 1) * 128], kap, qap, start=True, stop=True),
                            reads=allq + allk, writes=[("ps", sbank)], is_mm=True)

                def sm(st3=st3):
                    sbank = st3["s"]
                    pi = pcnt["p"] % 4
                    pcnt["p"] += 1
                    st3["p"] = pi
                    Pt = SCR[:, P_OFF + pi * 512:P_OFF + (pi + 1) * 512]
                    sch.add("act", lambda eng, Pt=Pt, sbank=sbank: eng.activation(out=Pt, in_=PS[sbank][:, :], func=AF.Exp, scale=0.125),
                            reads=[("ps", sbank)], writes=[("P", pi)])
                    sch.add("dve", lambda eng, Pt=Pt: eng.tensor_tensor(out=Pt, in0=Pt, in1=MASKS[:, 512:1024], op=ALU.mult),
                            reads=[("P", pi), ("masks",)], writes=[("P", pi)])

                def pv(g4=g4, st3=st3):
                    pi = st3["p"]
                    Pt = SCR[:, P_OFF + pi * 512:P_OFF + (pi + 1) * 512]
                    obank = 2 + pcnt["o"] % 2
                    pcnt["o"] += 1
                    for a in range(4):
                        r = g4 * 4 + a
                        Vt = SCR[:, V_OFF + vb * 4096 + (r * 2 + hh) * 128: V_OFF + vb * 4096 + (r * 2 + hh + 1) * 128]
                        sch.add("pe", lambda eng, obank=obank, a=a, Vt=Vt, Pt=Pt: eng.matmul(
                            PS[obank][:, a * 128:(a + 1) * 128], Vt, Pt[:, a * 128:(a + 1) * 128], start=True, stop=True),
                            reads=[("P", pi), ("V", vb, r)], writes=[("ps", obank)], is_mm=True)
                    dst = ACC[:, hh, :].rearrange("p (t r) -> p r t", r=16)[:, g4 * 4:(g4 + 1) * 4, :]
                    src = PS[obank][:, :].rearrange("p (a t) -> p a t", a=4)
                    sch.add("dve", lambda eng, dst=dst, src=src: eng.tensor_tensor(out=dst, in0=src, in1=dst, op=ALU.add),
                            reads=[("ps", obank), ("acc", hh)], writes=[("acc", hh)])

                out.append(Unit(qk, sm, pv))
            return out

        def vproj_units(vb, dil, qb):
            out = []
            L = S // dil
            ntile = L // 128
            tiles = []
            for r in range(dil):
                for kt in range(ntile):
                    tiles.append(slice(r + dil * 128 * kt, r + dil * (128 * kt + 127) + 1, dil))
            for t0 in range(0, 16, 4):
                stv = {}

                def qk(t0=t0, stv=stv):
                    sbank = pcnt["s"] % 2
                    pcnt["s"] += 1
                    stv["s"] = sbank
                    pvb = PS[sbank][:, 0:256].bitcast(BF16)
                    for a in range(4):
                        tk = tiles[t0 + a]
                        sch.add("pe", lambda eng, o=pvb[:, a * 128:(a + 1) * 128], i=VTS[qb][:, tk]: eng.transpose(o, i, IDENT),
                                reads=[("vt", qb), ("masks",)], writes=[("ps", sbank)], is_mm=True)

                def sm(t0=t0, stv=stv):
                    sbank = stv["s"]
                    pvb = PS[sbank][:, 0:256].bitcast(BF16)
                    Vb = SCR[:, V_OFF + vb * 4096: V_OFF + (vb + 1) * 4096]
                    dst = bass.AP(Vb.tensor, Vb.offset + t0 * 256, [list(Vb.ap[0]), [256, 4], [192, 2], [1, 64]])
                    src = bass.AP(pvb.tensor, pvb.offset, [list(pvb.ap[0]), [128, 4], [64, 2], [1, 64]])
                    sch.add("act", lambda eng, dst=dst, src=src: eng.activation(out=dst, in_=src, func=AF.Copy),
                            reads=[("ps", sbank)], writes=[("V", vb, t0 + a) for a in range(4)])

                out.append(Unit(qk, sm, None))
            return out

        def attn_units(p):
            qb = p % 2
            U = []
            for bi, dil in enumerate((1, 4, 16)):
                vb = bi % 2
                U += vproj_units(vb, dil, qb)
                for hh in range(2):
                    if dil == 16:
                        U += branch3_units(hh, vb, qb)
                    else:
                        U += branch_units(hh, S // dil, dil, vb, bi == 0, qb)
            return U

        bcnt = {"b": 0}

        def bset():
            i = bcnt["b"] % 2
            bcnt["b"] += 1
            return (4, 5) if i == 0 else (6, 7)

        evc = {"n": 0}

        def outproj_tiles(two=False):
            tiles = []
            holder = {}
            for dc in range(8):
                for tt in range(NT):
                    ts = slice(tt * 512, (tt + 1) * 512)
                    stt = {}

                    def pe(dc=dc, tt=tt, ts=ts, stt=stt):
                        if "slot" not in holder:
                            holder["slot"] = next_job("wout")
                            if two:
                                holder["slot2"] = next_job("wout", hold=1)
                        slot = holder["slot"]
                        oi = bset()[0]
                        stt["b"] = oi
                        sch.add("pe", lambda eng, oi=oi, slot=slot, dc=dc, ts=ts: eng.matmul(
                            PS[oi][:, :], WR[:, slot, dc * 128:(dc + 1) * 128], SCR[:, MIX_OFF + ts.start: MIX_OFF + ts.stop], start=True, stop=(not two)),
                            reads=[("w", slot), ("mixed", tt)], writes=[("ps", oi)], is_mm=True)
                        if two:
                            slot2 = holder["slot2"]
                            sch.add("pe", lambda eng, oi=oi, slot2=slot2, dc=dc, ts=ts: eng.matmul(
                                PS[oi][:, :], WR[:, slot2, dc * 128:(dc + 1) * 128], SCR[:, Q2_OFF + ts.start: Q2_OFF + ts.stop], start=False, stop=True),
                                reads=[("w", slot2), (("q", 1), tt)], writes=[("ps", oi)], is_mm=True)

                    def ev(dc=dc, tt=tt, ts=ts, stt=stt):
                        oi = stt["b"]
                        k = evc["n"]
                        evc["n"] += 1
                        if k % 3 == 0:
                            sch.add("dve", lambda eng, oi=oi, dc=dc, ts=ts: eng.tensor_tensor(
                                out=X[:, dc, ts], in0=PS[oi][:, :], in1=X[:, dc, ts], op=ALU.add),
                                reads=[("ps", oi), ("x", dc, tt)], writes=[("x", dc, tt)])
                            return
                        ti = cnt["tmpf"] % NTMPF
                        cnt["tmpf"] += 1
                        sch.add("act", lambda eng, oi=oi, ti=ti: eng.activation(out=TMPF[:, ti, :], in_=PS[oi][:, :], func=AF.Copy),
                                reads=[("ps", oi)], writes=[("tmpf", ti)])
                        sch.add("pool", lambda eng, ti=ti, dc=dc, ts=ts: eng.tensor_tensor(
                            out=X[:, dc, ts], in0=TMPF[:, ti, :], in1=X[:, dc, ts], op=ALU.add),
                            reads=[("tmpf", ti), ("x", dc, tt)], writes=[("x", dc, tt)])

                    tiles.append((pe, ev))
            return tiles

        def qkproj_tiles(which, qb):
            tiles = []
            holder = {}
            OFF = qk_off(qb)[which]
            name = ("q", qb) if which == 0 else ("k", qb)
            for tt in range(NT):
                ts = slice(tt * 512, (tt + 1) * 512)
                stt = {}

                def pe(tt=tt, ts=ts, stt=stt):
                    if "slot" not in holder:
                        holder["slot"] = next_job("qk")
                    slot = holder["slot"]
                    bm, bs = bset()
                    stt["b"] = (bm, bs)
                    for kc in range(KC):
                        sch.add("pe", lambda eng, slot=slot, kc=kc, ts=ts, bm=bm: eng.matmul(
                            PS[bm][:, :], WR[:, slot, kc * 128:(kc + 1) * 128], H[:, kc, ts], start=(kc == 0), stop=(kc == KC - 1)),
                            reads=[("w", slot), ("h", kc, tt)], writes=[("ps", bm)], is_mm=True)
                    for kc in range(KC):
                        sch.add("pe", lambda eng, slot=slot, kc=kc, ts=ts, bs=bs: eng.matmul(
                            PS[bs][:, :], WR[:, slot, 1024 + kc * 128:1024 + (kc + 1) * 128], H[:, kc, ts], start=(kc == 0), stop=(kc == KC - 1)),
                            reads=[("ws", slot), ("h", kc, tt)], writes=[("ps", bs)], is_mm=True)

                def ev(tt=tt, ts=ts, stt=stt):
                    bm, bs = stt["b"]
                    ti = cnt["tmpf"] % NTMPF
                    cnt["tmpf"] += 1
                    sch.add("dve", lambda eng, ti=ti, ts=ts, bm=bm: eng.tensor_tensor(out=TMPF[:, ti, :], in0=PS[bm][:, :], in1=ROPC[:, ts], op=ALU.mult),
                            reads=[("ps", bm), ("ropc", tt)], writes=[("tmpf", ti)])
                    sch.add("dve", lambda eng, ts=ts, bs=bs: eng.tensor_tensor(out=RSTD[:, :], in0=PS[bs][:, :], in1=ROPS[:, ts], op=ALU.mult),
                            reads=[("ps", bs), ("rops", tt)], writes=[("rstd",)])
                    sch.add("pool", lambda eng, ti=ti, ts=ts: eng.tensor_tensor(
                        out=SCR[:, OFF + ts.start:OFF + ts.stop], in0=TMPF[:, ti, :], in1=RSTD[:, :], op=ALU.add),
                        reads=[("tmpf", ti), ("rstd",)], writes=[(name, tt)])

                tiles.append((pe, ev))
            return tiles

        def vT_tiles(qb):
            tiles = []
            holder = {}
            for tt in range(NT):
                ts = slice(tt * 512, (tt + 1) * 512)
                stt = {}

                def pe(tt=tt, ts=ts, stt=stt):
                    if "slot" not in holder:
                        holder["slot"] = next_job("v")
                    slot = holder["slot"]
                    bv = bset()[0]
                    stt["b"] = bv
                    for kc in range(KC):
                        sch.add("pe", lambda eng, slot=slot, kc=kc, ts=ts, bv=bv: eng.matmul(
                            PS[bv][:, :], WR[:, slot, kc * 128:(kc + 1) * 128], H[:, kc, ts], start=(kc == 0), stop=(kc == KC - 1)),
                            reads=[("w", slot), ("h", kc, tt)], writes=[("ps", bv)], is_mm=True)

                def ev(tt=tt, ts=ts, stt=stt):
                    bv = stt["b"]
                    sch.add("act", lambda eng, ts=ts, bv=bv: eng.activation(out=VTS[qb][:, ts], in_=PS[bv][:, :], func=AF.Copy),
                            reads=[("ps", bv)], writes=[("vt", qb)])

                tiles.append((pe, ev))
            return tiles

        def proj_tiles(p):
            qb = p % 2
            return qkproj_tiles(0, qb) + qkproj_tiles(1, qb) + vT_tiles(qb)

        def run_tiles(tiles):
            for j, (pe, ev) in enumerate(tiles):
                pe()
                if j > 0:
                    tiles[j - 1][1]()
            if tiles:
                tiles[-1][1]()

        def emit_finalize():
            for hh in range(2):
                npr = slice(hh * 64, hh * 64 + 64)
                dpr = slice((1 - hh) * 64, (1 - hh) * 64 + 64)
                for tt in range(NT):
                    cs = slice(tt * 512, (tt + 1) * 512)
                    ti = cnt["tmpf"] % NTMPF
                    cnt["tmpf"] += 1
                    sch.add("dve", lambda eng, npr=npr, dpr=dpr, cs=cs, hh=hh, ti=ti: eng.reciprocal(out=TMPF[npr, ti, :], in_=ACC[dpr, hh, cs]),
                            reads=[("acc", hh)], writes=[("tmpf", ti)])
                    sch.add("pool", lambda eng, npr=npr, cs=cs, hh=hh, ti=ti: eng.tensor_tensor(
                        out=SCR[npr, MIX_OFF + cs.start:MIX_OFF + cs.stop], in0=ACC[npr, hh, cs], in1=TMPF[npr, ti, :], op=ALU.mult),
                        reads=[("acc", hh), ("tmpf", ti)], writes=[("mixed", tt)])

        def emit_mix(l):
            emit_norm(l * 3 + 1)
            emit_pool(l)
            scr_barrier()
            for vb in range(2):
                Vb = SCR[:, V_OFF + vb * 4096: V_OFF + (vb + 1) * 4096]
                dst = bass.AP(Vb.tensor, Vb.offset + 64, [list(Vb.ap[0]), [256, 16], [1, 128]])
                sch.add("pool", lambda eng, dst=dst: eng.memset(dst, 1.0),
                        writes=[("V", vb, t) for t in range(16)])
            run_tiles(proj_tiles(0))
            for p in range(6):
                B = []
                if p == 0:
                    B += outproj_tiles(two=True)
                if p > 0:
                    B += outproj_tiles()
                if p < 5:
                    B += proj_tiles(p + 1)
                U = attn_units(p)
                n = len(U)
                nb = len(B)
                for i in range(-2, max(n, nb - 1)):
                    if i + 2 < n:
                        U[i + 2].qk()
                    if 0 <= i + 1 < n:
                        U[i + 1].sm()
                    j = i + 2
                    if j < nb:
                        B[j][0]()
                    if 0 <= j - 1 < nb:
                        B[j - 1][1]()
                    if 0 <= i < n and U[i].pv is not None:
                        U[i].pv()
                emit_finalize()
            run_tiles(outproj_tiles())

        done = False
        for l in range(NL):
            emit_ffn(l, 0)
            if stop_after == ("ffn1", l):
                done = True
                break
            emit_mix(l)
            if stop_after == ("mix", l):
                done = True
                break
            emit_ffn(l, 1)

        if apply_final and not done:
            for tt in range(NT):
                ts = slice(tt * 512, (tt + 1) * 512)
                for kc in range(KC):
                    i = cnt["sq"] % 2
                    cnt["sq"] += 1
                    sch.add("act", lambda eng, i=i, kc=kc, ts=ts: eng.activation(out=SQ[:, i, :], in_=X[:, kc, ts], func=AF.Square),
                            reads=[("x", kc, tt)], writes=[("sq", i)])
                    sch.add("pe", lambda eng, i=i, kc=kc: eng.matmul(PS[6][:, :], ONES[:, :], SQ[:, i, :], start=(kc == 0), stop=(kc == KC - 1)),
                            reads=[("sq", i), ("ones",)], writes=[("ps", 6)], is_mm=True)
                sch.add("dve", lambda eng: eng.tensor_scalar(out=RSTD[:, :], in0=PS[6][:, :], scalar1=float(1.0 / D), scalar2=float(EPS), op0=ALU.mult, op1=ALU.add),
                        reads=[("ps", 6)], writes=[("rstd",)])
                sch.add("act", lambda eng: eng.activation(out=RSTD[:, :], in_=RSTD[:, :], func=AF.Sqrt),
                        reads=[("rstd",)], writes=[("rstd",)])
                sch.add("dve", lambda eng: eng.reciprocal(out=RSTD[:, :], in_=RSTD[:, :]),
                        reads=[("rstd",)], writes=[("rstd",)])
                for kc in range(KC):
                    sch.add("dve", lambda eng, kc=kc, ts=ts: eng.scalar_tensor_tensor(
                        out=X[:, kc, ts], in0=X[:, kc, ts], scalar=cc(C_GAIN + 12 * 8 + kc), in1=RSTD[:, :], op0=ALU.mult, op1=ALU.mult),
                        reads=[("x", kc, tt), ("rstd",), ("const",)], writes=[("x", kc, tt)])
        sch.add("sp", lambda eng: eng.dma_start(out=out_d, in_=X[:, :, :].rearrange("p k t -> p (k t)")),
                reads=[("x", kc, tt) for kc in range(KC) for tt in range(NT)], writes=[("out",)], dma="out")
        sch.add("sp", lambda eng: None, reads=[("out",)])

        with nc.Block() as block:
            sch.emit(nc, block, esems, dsems)
    return nc


def _consts():
    c = np.zeros((128, NCONST), np.float32)
    inv_freq = (ROPE_THETA ** (-np.arange(0, 16, 2, dtype=np.float32) / np.float32(16))).astype(np.float32)
    for p in range(128):
        d = p % 64
        if d < 16:
            c[p, C_INVF] = inv_freq[d % 8]
            c[p, C_SGN] = -1.0 if d < 8 else 1.0
    c[:, C_HALFPI] = np.float32(np.pi / 2)
    pos = np.arange(S)
    for c2 in range(2):
        for half in range(2):
            w = POOL_WINDOWS[c2 * 2 + half]
            lo = np.maximum(pos - w // 2, 0)
            hi = np.minimum(pos + w - 1 - w // 2, S - 1)
            cntv = (hi - lo + 1).astype(np.float32)
            pr = slice(half * 64, half * 64 + 64)
            c[pr, C_INVW + c2] = np.float32(1.0) / np.float32(w)
            c[pr, C_INVCNT + c2 * 16: C_INVCNT + c2 * 16 + 8] = (np.float32(1.0) / cntv[:8])[None, :]
            c[pr, C_INVCNT + c2 * 16 + 8: C_INVCNT + c2 * 16 + 16] = (np.float32(1.0) / cntv[-8:])[None, :]
    i = np.arange(128)[:, None]
    j = np.arange(256)[None, :]
    m12 = ((j >= i) & (j <= i + 128)).astype(np.float32)
    j3 = np.arange(128)[None, :]
    m3 = (np.abs(i - j3) <= 64).astype(np.float32)
    masks = np.concatenate([m12, m12, m3, m3, m3, m3, np.eye(128, dtype=np.float32)], axis=1).astype(np.float32)
    return c, masks


def _kc_layout(w):
    n = w.shape[1]
    return np.ascontiguousarray(w.reshape(KC, 128, n).transpose(1, 0, 2).reshape(128, KC * n))


def prepare_shared(inp):
    f32 = np.float32
    consts, masks = _consts()
    gains = [None] * 13
    for l in range(DEPTH):
        gains[l * 3 + 0] = inp["ffn1_norm"][l]
        gains[l * 3 + 1] = inp["mix_norm"][l]
        gains[l * 3 + 2] = inp["ffn2_norm"][l]
    gains[12] = inp["final_norm"]
    for n in range(13):
        consts[:, C_GAIN + n * 8: C_GAIN + (n + 1) * 8] = np.asarray(gains[n], f32).reshape(KC, 128).T
    ps = np.asarray(inp["pool_scale"], f32)
    for l in range(DEPTH):
        for c2 in range(2):
            consts[:, C_PSCALE + l * 2 + c2] = ps[l, c2 * 128:(c2 + 1) * 128]
    pw = np.asarray(inp["pool_w"], f32)
    poolbd = np.zeros((128, DEPTH * 2 * 128), f32)
    for l in range(DEPTH):
        for c2 in range(2):
            blk = np.zeros((128, 128), f32)
            blk[0:64, 0:64] = pw[l, c2 * 2]
            blk[64:128, 64:128] = pw[l, c2 * 2 + 1]
            poolbd[:, (l * 2 + c2) * 128:(l * 2 + c2 + 1) * 128] = blk
    wgu = np.empty((DEPTH * 2 * FC * 128, 2048), f32)
    wd = np.zeros((DEPTH * 2 * 3 * 8 * 128, 1024), f32)
    for l in range(DEPTH):
        for f in range(2):
            pre = "ffn1" if f == 0 else "ffn2"
            g = np.asarray(inp[pre + "_w_gate"][l], f32)
            u = np.asarray(inp[pre + "_w_up"][l], f32)
            dn = np.asarray(inp[pre + "_w_down"][l], f32)
            for j in range(FC):
                r0 = ((l * 2 + f) * FC + j) * 128
                wgu[r0:r0 + 128, 0:1024] = _kc_layout(g[:, j * 128:(j + 1) * 128])
                wgu[r0:r0 + 128, 1024:2048] = _kc_layout(u[:, j * 128:(j + 1) * 128])
            j0 = 0
            for gi, ng in enumerate(F_GROUPS):
                for dc in range(8):
                    r0 = (((l * 2 + f) * 3 + gi) * 8 + dc) * 128
                    blk = dn[j0 * 128:(j0 + ng) * 128, dc * 128:(dc + 1) * 128]
                    wd[r0:r0 + 128, 0:ng * 128] = blk.reshape(ng, 128, 128).transpose(1, 0, 2).reshape(128, ng * 128)
                j0 += ng
    win = np.asarray(inp["w_in"], f32)
    wqk = np.empty((DEPTH * 6 * 2 * 128, 1024), f32)
    wv = np.empty((DEPTH * 6 * 128, 1024), f32)
    wpool = np.empty((DEPTH * 2 * 128, 1024), f32)
    for l in range(DEPTH):
        for c2 in range(2):
            r0 = (l * 2 + c2) * 128
            wpool[r0:r0 + 128] = _kc_layout(win[l][:, c2 * 128:(c2 + 1) * 128])
        for p in range(6):
            for which in range(2):
                r0 = ((l * 6 + p) * 2 + which) * 128
                c0 = 256 + which * 768 + p * 128
                wqk[r0:r0 + 128] = _kc_layout(win[l][:, c0:c0 + 128])
            r0 = (l * 6 + p) * 128
            c0 = 256 + 1536 + p * 128
            wv[r0:r0 + 128] = _kc_layout(win[l][:, c0:c0 + 128])
    wout = np.ascontiguousarray(np.asarray(inp["w_out"], f32).reshape(DEPTH * 8 * 128, 1024))
    return {"consts": consts, "masks": masks, "poolbd": poolbd, "wgu": wgu, "wd": wd,
            "wqk": wqk, "wv": wv, "wpool": wpool, "wout": wout}


def x_to_dev(xb):
    return np.ascontiguousarray(np.asarray(xb, np.float32).T.reshape(KC, 128, S).transpose(1, 0, 2).reshape(128, KC * S))


def dev_to_x(o):
    return np.ascontiguousarray(o.reshape(128, KC, S).transpose(1, 0, 2).reshape(D, S).T)


_CACHE = {}


def kernel(**inputs):
    shared = prepare_shared(inputs)
    x = np.asarray(inputs["x"], np.float32)
    pos = np.asarray(inputs["positions"], np.int32)
    if "nc" not in _CACHE:
        _CACHE["nc"] = build_program()
    nc = _CACHE["nc"]
    in_maps = []
    for b in range(8):
        m = dict(shared)
        m["xT"] = x_to_dev(x[b])
        m["pos"] = np.ascontiguousarray(pos[b].reshape(1, S))
        in_maps.append(m)
    res = run_bass_kernel_spmd(nc, in_maps, core_ids=list(range(8)))
    out = np.stack([dev_to_x(np.asarray(res.results[b]["outT"])) for b in range(8)], axis=0)
    return out.astype(np.float32)
```
